# Optimizing a Trainium2 kernel written in Bass

```python
import math
import jax, jax.numpy as jnp
from jax import lax
import numpy as np

D_MODEL = 2048
BATCH = 2
SEQ = 4096
DEPTH = 1

MIX_WIDTH = D_MODEL
ATTN_WIDTH = MIX_WIDTH // 2
CONV_WIDTH = MIX_WIDTH - ATTN_WIDTH
HEAD_DIM = 128
N_HEADS = ATTN_WIDTH // HEAD_DIM
CONV_GROUP = 128
N_CONV_GROUPS = CONV_WIDTH // CONV_GROUP
CONV_K = 3
IN_WIDTH = 3 * ATTN_WIDTH + 3 * CONV_WIDTH
MOBA_BLOCK = 256
MOBA_TOPK = 3
Q_CHUNK = 32
D_FF = 5632
FFN_CONV_K = 3
EPS = 1e-6
NEG_INF = -1e30

kernel_name = "hybrid_moba_shortconv_convffn_block"


def rms_norm(x, g):
    x32 = x.astype(jnp.float32)
    y = x32 * lax.rsqrt(jnp.mean(x32 * x32, axis=-1, keepdims=True) + EPS)
    return (y * g.astype(jnp.float32)).astype(x.dtype)


def modulate(h, shift, scale):
    return h * (1.0 + scale[:, None, :]) + shift[:, None, :]


def causal_dwconv(x, w, b):
    K, C = w.shape
    y = lax.conv_general_dilated(
        x, w[:, None, :].astype(x.dtype), window_strides=(1,), padding=[(K - 1, 0)],
        dimension_numbers=("NWC", "WIO", "NWC"), feature_group_count=C)
    return y + b.astype(x.dtype)


def moba_attention(q, k, v):
    B, H, S, HD = q.shape
    s_pad = ((S + MOBA_BLOCK - 1) // MOBA_BLOCK) * MOBA_BLOCK
    pad = s_pad - S
    padw = ((0, 0), (0, 0), (0, pad), (0, 0))
    q = jnp.pad(q, padw)
    k = jnp.pad(k, padw)
    v = jnp.pad(v, padw)
    nb = s_pad // MOBA_BLOCK
    n_sel = min(MOBA_TOPK, nb)
    scale = 1.0 / math.sqrt(HD)

    kb = k.reshape(B, H, nb, MOBA_BLOCK, HD)
    vb = v.reshape(B, H, nb, MOBA_BLOCK, HD)
    kmean = jnp.mean(kb.astype(jnp.float32), axis=3)
    gate = jnp.einsum("bhsd,bhnd->bhsn", q.astype(jnp.float32), kmean)
    q_blk = jnp.arange(s_pad) // MOBA_BLOCK
    past = jnp.arange(nb)[None, :] < q_blk[:, None]
    gate = jnp.where(past[None, None], gate, NEG_INF)
    _, sel = lax.top_k(gate, n_sel)
    sel_valid = jnp.arange(n_sel)[None, :] < q_blk[:, None]

    base = (jnp.arange(B)[:, None] * H + jnp.arange(H)[None, :]) * nb
    flat_sel = sel + base[:, :, None, None]
    kb_flat = kb.reshape(B * H * nb, MOBA_BLOCK, HD)
    vb_flat = vb.reshape(B * H * nb, MOBA_BLOCK, HD)

    nq = s_pad // Q_CHUNK
    q_ch = q.reshape(B, H, nq, Q_CHUNK, HD).transpose(2, 0, 1, 3, 4)
    sel_ch = flat_sel.reshape(B, H, nq, Q_CHUNK, n_sel).transpose(2, 0, 1, 3, 4)
    valid_ch = sel_valid.reshape(nq, Q_CHUNK, n_sel)
    ids = jnp.arange(nq)

    def step(args):
        qc, selc, validc, ci = args
        blk = (ci * Q_CHUNK) // MOBA_BLOCK
        k_own = lax.dynamic_index_in_dim(kb, blk, axis=2, keepdims=False)
        v_own = lax.dynamic_index_in_dim(vb, blk, axis=2, keepdims=False)
        qpos = ci * Q_CHUNK + jnp.arange(Q_CHUNK)
        kpos = blk * MOBA_BLOCK + jnp.arange(MOBA_BLOCK)
        s_own = jnp.einsum("bhqd,bhkd->bhqk", qc, k_own).astype(jnp.float32) * scale
        s_own = jnp.where((kpos[None, :] <= qpos[:, None])[None, None], s_own, NEG_INF)
        k_sel = kb_flat[selc]
        v_sel = vb_flat[selc]
        s_sel = jnp.einsum("bhqd,bhqnkd->bhqnk", qc, k_sel).astype(jnp.float32) * scale
        s_sel = jnp.where(validc[None, None, :, :, None], s_sel, NEG_INF)
        s_all = jnp.concatenate(
            [s_own, s_sel.reshape(B, H, Q_CHUNK, n_sel * MOBA_BLOCK)], axis=-1)
        p = jax.nn.softmax(s_all, axis=-1)
        p_own = p[..., :MOBA_BLOCK].astype(v.dtype)
        p_sel = p[..., MOBA_BLOCK:].reshape(B, H, Q_CHUNK, n_sel, MOBA_BLOCK).astype(v.dtype)
        return (jnp.einsum("bhqk,bhkd->bhqd", p_own, v_own)
                + jnp.einsum("bhqnk,bhqnkd->bhqd", p_sel, v_sel))

    out = lax.map(step, (q_ch, sel_ch, valid_ch, ids))
    out = out.transpose(1, 2, 0, 3, 4).reshape(B, H, s_pad, HD)
    return out[:, :, :S]


def setup_inputs(seed: int = 0) -> dict:
    key = jax.random.key(seed)
    ks = jax.random.split(key, 20)
    f32 = jnp.float32
    nrm = lambda k, shape, s: jax.random.normal(k, shape, f32) * s
    gain = lambda k, shape: 1.0 + 0.02 * jax.random.normal(k, shape, f32)
    return {
        "x": jax.random.normal(ks[0], (BATCH, SEQ, D_MODEL), f32),
        "c": jax.random.normal(ks[1], (BATCH, D_MODEL), f32),
        "w_ada": nrm(ks[2], (DEPTH, D_MODEL, 6 * D_MODEL), 0.5 * D_MODEL ** -0.5),
        "b_ada": nrm(ks[3], (DEPTH, 6 * D_MODEL), 0.02),
        "norm1_g": gain(ks[4], (DEPTH, D_MODEL)),
        "w_in": nrm(ks[5], (DEPTH, D_MODEL, IN_WIDTH), D_MODEL ** -0.5),
        "q_norm_g": gain(ks[6], (DEPTH, HEAD_DIM)),
        "k_norm_g": gain(ks[7], (DEPTH, HEAD_DIM)),
        "conv_w": nrm(ks[8], (DEPTH, CONV_K, CONV_WIDTH), CONV_K ** -0.5),
        "conv_b": nrm(ks[9], (DEPTH, CONV_WIDTH), 0.02),
        "attn_out_g": gain(ks[10], (DEPTH, ATTN_WIDTH)),
        "conv_out_g": gain(ks[11], (DEPTH, CONV_WIDTH)),
        "w_out": nrm(ks[12], (DEPTH, MIX_WIDTH, D_MODEL), MIX_WIDTH ** -0.5),
        "norm2_g": gain(ks[13], (DEPTH, D_MODEL)),
        "w_ffn_up": nrm(ks[14], (DEPTH, D_MODEL, 2 * D_FF), D_MODEL ** -0.5),
        "ffn_conv_w": nrm(ks[15], (DEPTH, FFN_CONV_K, 2 * D_FF), FFN_CONV_K ** -0.5),
        "ffn_conv_b": nrm(ks[16], (DEPTH, 2 * D_FF), 0.02),
        "w_ffn_down": nrm(ks[17], (DEPTH, D_FF, D_MODEL), D_FF ** -0.5),
    }


def reference(x, c, w_ada, b_ada, norm1_g, w_in, q_norm_g, k_norm_g, conv_w, conv_b,
              attn_out_g, conv_out_g, w_out, norm2_g, w_ffn_up, ffn_conv_w, ffn_conv_b,
              w_ffn_down):
    B, S, D = x.shape
    for l in range(DEPTH):
        mod = jax.nn.silu(c) @ w_ada[l] + b_ada[l]
        sh1, sc1, g1, sh2, sc2, g2 = jnp.split(mod, 6, axis=-1)

        h = modulate(rms_norm(x, norm1_g[l]), sh1, sc1)
        proj = h @ w_in[l]
        q, k, v, gb, gc, xt = jnp.split(
            proj, [ATTN_WIDTH, 2 * ATTN_WIDTH, 3 * ATTN_WIDTH,
                   3 * ATTN_WIDTH + CONV_WIDTH, 3 * ATTN_WIDTH + 2 * CONV_WIDTH], axis=-1)

        q = rms_norm(q.reshape(B, S, N_HEADS, HEAD_DIM), q_norm_g[l]).transpose(0, 2, 1, 3)
        k = rms_norm(k.reshape(B, S, N_HEADS, HEAD_DIM), k_norm_g[l]).transpose(0, 2, 1, 3)
        v = v.reshape(B, S, N_HEADS, HEAD_DIM).transpose(0, 2, 1, 3)
        attn = moba_attention(q, k, v).transpose(0, 2, 1, 3).reshape(B, S, ATTN_WIDTH)

        conv = gb * causal_dwconv(gc * xt, conv_w[l], conv_b[l])

        mixed = jnp.concatenate(
            [rms_norm(attn, attn_out_g[l]), rms_norm(conv, conv_out_g[l])], axis=-1)
        x = x + g1[:, None, :] * (mixed @ w_out[l])

        h2 = modulate(rms_norm(x, norm2_g[l]), sh2, sc2)
        u = causal_dwconv(h2 @ w_ffn_up[l], ffn_conv_w[l], ffn_conv_b[l])
        u_gate, u_up = jnp.split(u, 2, axis=-1)
        ffn = (jax.nn.silu(u_gate) * u_up) @ w_ffn_down[l]
        x = x + g2[:, None, :] * ffn
    return x
```

```python
import math
from contextlib import ExitStack

import numpy as np
import ml_dtypes

import concourse.bass as bass
import concourse.mybir as mybir
from concourse.bass_utils import run_bass_kernel_spmd

F32 = mybir.dt.float32
BF16 = mybir.dt.bfloat16
AF = mybir.ActivationFunctionType
ALU = mybir.AluOpType

D = 2048
NFC = 16
TOK = 1024
NH = 8
DFF = 5632
NFF = 44
EPS = 1e-6
NEG = -1.0e30
CHUNK = 11
NCHUNK = 4

C_N1G, C_N2G, C_GQ, C_GK = 0, 16, 32, 33
C_CW, C_CB, C_AG, C_CG = 34, 58, 66, 74
C_FW, C_FB = 82, 346
C_PB, C_P01, C_SELW, C_BSEL, C_HM, C_ID = 434, 498, 562, 566, 568, 576
NCF = 704
B_ID, B_ONES, B_TRI = 0, 128, 256
NCB = 768

DEBUG = {}
STOP = 0


class _Stop(Exception):
    pass


class Buf:
    __slots__ = ("name", "last_w", "readers")

    def __init__(self, name):
        self.name = name
        self.last_w = None
        self.readers = []


def alias(new_bufs, old_bufs):
    toks = []
    for o in old_bufs:
        if o.last_w is not None:
            toks.append(o.last_w)
        toks.extend(o.readers)
    for n in new_bufs:
        n.readers = list(n.readers) + toks


class Sched:
    ENG = ("pe", "act", "dve", "pool", "sp")

    def __init__(self):
        self.prog = {e: [] for e in self.ENG}
        self.count = {e: 0 for e in self.ENG}
        self.known = {e: {} for e in self.ENG}
        self.dma_sems = {}
        self.n_dma_sems = 0

    def _deps(self, eng, reads, writes):
        need = {}

        def add(tok, same_ok):
            if tok is None:
                return
            s, v = tok
            if s == eng and not same_ok:
                return
            if need.get(s, 0) < v:
                need[s] = v

        same_raw = eng in ("act", "dve", "pool")
        for b in reads:
            add(b.last_w, same_raw)
        for b in writes:
            add(b.last_w, same_raw)
            for t in b.readers:
                add(t, False)
        waits = []
        kn = self.known[eng]
        for s, v in need.items():
            if kn.get(s, 0) < v:
                kn[s] = v
                waits.append((s, v))
        return waits

    def op(self, eng, fn, reads=(), writes=()):
        waits = self._deps(eng, reads, writes)
        self.count[eng] += 1
        tok = (eng, self.count[eng])
        self.prog[eng].append((waits, fn, (eng, 1)))
        for b in reads:
            b.readers.append(tok)
        for b in writes:
            b.last_w = tok
            b.readers = []
        return tok

    def dma(self, queue, fn, reads=(), writes=(), sem=None, inc=16):
        waits = self._deps(queue, reads, writes)
        if sem not in self.dma_sems:
            self.dma_sems[sem] = 0
        self.dma_sems[sem] += inc
        tok = (sem, self.dma_sems[sem])
        self.prog[queue].append((waits, fn, (sem, inc)))
        for b in reads:
            b.readers.append(tok)
        for b in writes:
            b.last_w = tok
            b.readers = []
        return tok

    def final_wait(self, queue, toks):
        waits = []
        for s, v in toks:
            waits.append((s, v))
        self.prog[queue].append((waits, None, None))


def build_program():
    nc = bass.Bass("TRN2", target_bir_lowering=False)
    S = Sched()

    declared = []

    def din(name, shape, dt=F32, lvl=0):
        if STOP != 0 and STOP < lvl:
            return None
        declared.append(name)
        return nc.dram_tensor(name, list(shape), dt, kind="ExternalInput").ap()

    def stop_at(level):
        if STOP == level:
            raise _Stop()

    xT_d = din("xT", [128, NFC, TOK])
    xhT_d = din("xhT", [128, NFC, 8])
    cT_d = din("cT", [128, NFC, 2])
    wada_d = din("wada", [128, 2, NFC, 1536])
    bada_d = din("bada", [2, 3072])
    cf_d = din("cf", [128, NCF])
    cb_d = din("cb16", [128, NCB], BF16)
    WSH = {}
    for nm_, ntile_, lvl_ in (("w_in_t", 48, 3), ("w_out_t", 16, 6), ("w_up_t", 88, 9), ("w_down_t", NFF, 9)):
        rows_ = ntile_ * 128
        sh_ = din(nm_, [rows_, 2048], lvl=lvl_)
        if sh_ is None:
            WSH[nm_] = None
            continue
        WSH[nm_] = (sh_, nc.dram_tensor(nm_ + "_b", [rows_, 2048], F32),
                    nc.dram_tensor(nm_ + "_f", [ntile_ * 128, 2048], F32), rows_)

    def wfull(nm_):
        if WSH[nm_] is None:
            return None
        return WSH[nm_][0].rearrange("(t p) x -> t p x", p=128)
    win_d, wout_d, wup_d, wdn_d = wfull("w_in_t"), wfull("w_out_t"), wfull("w_up_t"), wfull("w_down_t")
    out_d = nc.dram_tensor("outT", [128, NFC, TOK], F32, kind="ExternalOutput").ap()
    dbg_d = {}
    for k, (shape, dt) in DEBUG.items():
        dbg_d[k] = nc.dram_tensor("dbg_" + k, list(shape), dt, kind="ExternalOutput").ap()

    mod_loc = nc.dram_tensor("mod_loc", [2, 3072], F32)
    mod_g = nc.dram_tensor("mod_g", [8, 3072], F32)
    k_loc = [nc.dram_tensor("k_loc%d" % h_, [128, 1024], BF16) for h_ in range(NH)]
    k_g = [nc.dram_tensor("k_g%d" % h_, [512, 1024], BF16) for h_ in range(NH)]
    v_loc = [nc.dram_tensor("v_loc%d" % h_, [128, 1024], BF16) for h_ in range(NH)]
    v_g = [nc.dram_tensor("v_g%d" % h_, [512, 1024], BF16) for h_ in range(NH)]
    km_loc = nc.dram_tensor("km_loc", [128, 32], F32)
    km_g = nc.dram_tensor("km_g", [512, 32], F32)
    xh_loc = nc.dram_tensor("xh_loc", [128, 128], F32)
    xh_g = nc.dram_tensor("xh_g", [512, 128], F32)
    G8 = [list(range(8))]
    G4 = [[0, 1, 2, 3], [4, 5, 6, 7]]

    es = ExitStack()
    with es:
        def sb(name, shape, dt):
            return es.enter_context(nc.sbuf_tensor("sb_" + name, list(shape), dt))

        R1 = sb("R1", [128, 32768], BF16)
        R2 = sb("R2", [128, 16384], BF16)
        R3 = sb("R3", [128, 8, NFC, 128], BF16)
        R5 = sb("R5", [128, 24832], BF16)
        R4 = sb("R4", [128, 10272], BF16)
        cf = sb("cf", [128, NCF], F32)
        cb = sb("cb", [128, NCB], BF16)
        modT = sb("modT", [128, 96], F32)
        gsc = sb("gsc", [128, 40], F32)
        sT = sb("sT", [128, NFC, 2], BF16)
        cTs = sb("cTs", [128, NFC, 2], F32)
        hhT = sb("hhT", [128, NFC, 8], BF16)
        kmsum = sb("kmsum", [128, 32], F32)
        smallv = sb("smallv", [128, 64], F32)
        psum = [es.enter_context(nc.psum_tensor("ps%d" % i, [128, 512], F32)) for i in range(8)]
        PB = [Buf("ps%d" % i) for i in range(8)]

        def carve(region, off_bytes, shape, dt):
            size = 4 if dt == F32 else 2
            n = int(np.prod(shape[1:]))
            a = region[:, off_bytes // 2: off_bytes // 2 + n * size // 2]
            if dt == F32:
                a = a.bitcast(F32)
            if len(shape) == 2:
                return a
            names = " ".join("d%d" % i for i in range(len(shape) - 1))
            kw = {"d%d" % i: shape[i + 1] for i in range(len(shape) - 1)}
            return a.rearrange("p (%s) -> p %s" % (names, names), **kw)

        xT = carve(R1, 0, [128, NFC, TOK], F32)
        qT = carve(R1, 0, [128, NH, TOK], BF16)
        kT = carve(R1, 16384, [128, NH, TOK], BF16)
        vA = carve(R1, 32768, [128, NH, 8, 132], BF16)
        selb = carve(R1, 49664, [128, 8, NH, 16], F32)
        hT = carve(R2, 0, [128, NFC, TOK], BF16)
        attn = carve(R2, 0, [128, 8, TOK], F32)
        KH = [carve(R5, 0, [128, 16, 256], BF16), carve(R5, 8192, [128, 16, 256], BF16)]
        VH = [carve(R5, 16384, [128, 16, 2, 132], BF16), carve(R5, 16384 + 8448, [128, 16, 2, 132], BF16)]
        mixC = carve(R5, 33280, [128, 8, TOK], BF16)
        mixA = carve(R5, 0, [128, 8, TOK], BF16)
        actT = carve(R5, 0, [128, CHUNK, TOK], BF16)
        WD = [carve(R5, 22528, [128, CHUNK, 512], BF16), carve(R5, 22528 + 11264, [128, CHUNK, 512], BF16)]
        wa = carve(R5, 0, [128, NFC, 1536], BF16)
        ubuf = [carve(R4, 0, [128, 4, 258], F32), carve(R4, 4128, [128, 4, 258], F32)]
        ybuf = [carve(R4, 8256, [128, 4, 256], F32), carve(R4, 12352, [128, 4, 256], F32)]
        sg = carve(R4, 16448, [128, 4, 256], F32)
        sqb = [carve(R4, 16448, [128, 512], BF16), carve(R4, 17472, [128, 512], BF16)]
        rstd = [carve(R4, 18496, [128, 512], F32), carve(R4, 4128, [128, 512], F32)]
        modp = carve(R4, 0, [128, 1536], F32)
        badat = carve(R4, 6144, [128, 1536], F32)
        modgs = carve(R4, 12288, [128, 128], F32)
        pT = [carve(R4, 0, [128, 2, 256], BF16), carve(R4, 1024, [128, 2, 256], BF16)]
        acc = [carve(R4, 2048, [128, 2, 132], F32), carve(R4, 3104, [128, 2, 132], F32)]
        gm = carve(R4, 4160, [128, NH, 16], F32)
        top8 = carve(R4, 4672, [128, NH, 8], F32)
        attn_n = carve(R4, 4928, [128, TOK], BF16)
        kmall = carve(R4, 6976, [128, 4, 32], F32)
        kmT2 = carve(R4, 7488, [128, NH, 16], F32)
        kmhi = carve(R4, 8000, [128, NH, 16], BF16)
        kmlo = carve(R4, 8256, [128, NH, 16], BF16)
        kmtmp = carve(R4, 8512, [128, NH, 16], F32)
        xhpay = carve(R4, 0, [128, NFC, 4, 2], F32)
        xhg = carve(R4, 512, [128, 4, NFC, 4, 2], F32)
        xh2 = carve(R4, 2560, [128, NFC, 4, 2], F32)
        xhtmp = carve(R4, 3072, [128, NFC, 8], F32)
        xhsq = carve(R4, 3584, [128, NFC, 8], BF16)
        xhin = carve(R4, 0, [128, NFC, 8], F32)

        b_R1 = [Buf("R1_%d" % i) for i in range(16)]
        b_R2 = [Buf("R2_%d" % i) for i in range(16)]
        b_ring = [Buf("ring%d" % i) for i in range(2)]
        b_R4 = Buf("R4")
        b_cf, b_cb, b_modT, b_gsc = Buf("cf"), Buf("cb"), Buf("modT"), Buf("gsc")
        b_sT, b_cTs, b_hhT, b_kmsum, b_small = Buf("sT"), Buf("cTs"), Buf("hhT"), Buf("kmsum"), Buf("small")
        b_wa = Buf("wa")
        b_KH, b_VH = [Buf("kh0"), Buf("kh1")], [Buf("vh0"), Buf("vh1")]
        b_mixC = [Buf("mixC%d" % i) for i in range(8)]
        b_mixA = [Buf("mixA%d" % i) for i in range(8)]
        b_act = Buf("act")
        b_WD = [Buf("wd0"), Buf("wd1")]
        b_ubuf, b_ybuf, b_sg = [Buf("ub0"), Buf("ub1")], [Buf("yb0"), Buf("yb1")], Buf("sg")
        b_sqb, b_rstd = [Buf("sq0"), Buf("sq1")], [Buf("rs0"), Buf("rs1")]
        b_pT, b_acc = [Buf("pT0"), Buf("pT1")], [Buf("acc0"), Buf("acc1")]
        b_gm, b_top8, b_attn_n, b_km = Buf("gm"), Buf("top8"), Buf("attn_n"), Buf("km")
        b_dram = {n: Buf(n) for n in ("mod_loc", "mod_g", "km_loc", "km_g", "xh_loc", "xh_g")}
        for h_ in range(NH):
            for n_ in ("k_loc", "k_g", "v_loc", "v_g"):
                b_dram["%s%d" % (n_, h_)] = Buf("%s%d" % (n_, h_))
        b_xT = b_R1
        b_qT = [b_R1[h // 2] for h in range(8)]
        b_kT = [b_R1[4 + h // 2] for h in range(8)]
        b_vA = b_R1[8:13]
        b_sel = b_R1[12:14]
        b_hT = b_R2

        def col(c, n=1):
            return cf[:, c:c + n]

        ident_f = cf[:, C_ID:C_ID + 128]
        ident_b = cb[:, B_ID:B_ID + 128]
        ones_b = cb[:, B_ONES:B_ONES + 128]
        tri = cb[:, B_TRI:B_TRI + 512].rearrange("p (c q) -> p c q", c=2)

        E = {"pe": nc.tensor, "act": nc.scalar, "dve": nc.vector, "pool": nc.gpsimd, "sp": nc.sync}

        ring_state = {"n": 0, "w": "w_in_t"}

        def load_group(dram, g, wname=None):
            n = ring_state["n"]
            ring_state["n"] += 1
            grp = n % 2
            dst = R3[:, grp * 4:(grp + 1) * 4, :, :].rearrange("p s f m -> p s (f m)")
            src = dram[4 * g:4 * g + 4, :, :].rearrange("s p x -> p s x")
            S.dma("pool", lambda: nc.gpsimd.dma_start(out=dst, in_=src),
                  reads=[b_wf[ring_state["w"]]], writes=[b_ring[grp]], sem="ring%d" % grp)
            return grp

        def dbg(name, src_ap, bufs):
            if name in dbg_d:
                S.dma("sp", lambda: nc.sync.dma_start(out=dbg_d[name], in_=src_ap),
                      reads=bufs, sem="dbg_" + name)

        try:
            S.dma("sp", lambda: nc.sync.dma_start(out=cf[:, :], in_=cf_d), writes=[b_cf], sem="cf")
            S.dma("sp", lambda: nc.sync.dma_start(out=cb[:, :], in_=cb_d), writes=[b_cb], sem="cb")
            S.dma("sp", lambda: nc.sync.dma_start(out=cTs[:, :, :], in_=cT_d), writes=[b_cTs], sem="cT")
            b_bad = Buf("badat")
            for q4 in range(4):
                S.dma("sp", (lambda q4=q4: nc.sync.dma_start(
                    out=xT[:, :, q4 * 256:(q4 + 1) * 256], in_=xT_d[:, :, q4 * 256:(q4 + 1) * 256])),
                    writes=b_xT, sem="xT")
            b_wf = {nm_: Buf(nm_ + "_f") for nm_ in ("w_in_t", "w_out_t", "w_up_t", "w_down_t")}
            S.op("act", lambda: nc.scalar.activation(out=sT[:, :, :], in_=cTs[:, :, :], func=AF.Silu),
                 reads=[b_cTs], writes=[b_sT])
            for half in range(2):
                S.dma("sp", (lambda half=half: nc.sync.dma_start(
                    out=badat[0:2, :], in_=bada_d[:, half * 1536:(half + 1) * 1536])), writes=[b_bad], sem="bada")
                for q4 in range(4):
                    S.dma("pool", (lambda q4=q4, half=half: nc.gpsimd.dma_start(
                        out=wa[:, q4 * 4:(q4 + 1) * 4, :], in_=wada_d[:, half, q4 * 4:(q4 + 1) * 4, :])),
                        writes=[b_wa], sem="wa")
                for g in range(3):
                    def mm(g=g):
                        ins = None
                        for fc in range(NFC):
                            ins = nc.tensor.matmul(psum[g][0:2, :], lhsT=sT[:, fc, :],
                                                   rhs=wa[:, fc, g * 512:(g + 1) * 512],
                                                   start=(fc == 0), stop=(fc == NFC - 1))
                        return ins
                    S.op("pe", mm, reads=[b_sT, b_wa], writes=[PB[g]])
                    S.op("dve", (lambda g=g: nc.vector.tensor_tensor(
                        out=modp[0:2, g * 512:(g + 1) * 512], in0=psum[g][0:2, :],
                        in1=badat[0:2, g * 512:(g + 1) * 512], op=ALU.add)),
                        reads=[PB[g], b_bad], writes=[b_R4])
                S.dma("sp", (lambda half=half: nc.sync.dma_start(
                    out=mod_loc.ap()[:, half * 1536:(half + 1) * 1536], in_=modp[0:2, :])),
                    reads=[b_R4], writes=[b_dram["mod_loc"]], sem="modst")
            S.dma("pool", lambda: nc.gpsimd.collective_compute(
                "AllGather", ALU.bypass, replica_groups=G4,
                ins=[mod_loc.ap().opt()], outs=[mod_g.ap().opt()]),
                reads=[b_dram["mod_loc"]], writes=[b_dram["mod_g"]], sem="cc_mod", inc=1)
            for r in range(4):
                S.dma("sp", (lambda r=r: nc.sync.dma_start(
                    out=modgs[r * 24:(r + 1) * 24, :],
                    in_=mod_g.ap()[2 * r:2 * r + 1, :].rearrange("b (q p) -> (b q) p", p=128))),
                    reads=[b_dram["mod_g"]], writes=[b_R4], sem="modld")
            S.op("pe", lambda: nc.tensor.transpose(psum[3][:, 0:96], modgs[0:96, :], ident_f[0:96, 0:96]),
                 reads=[b_R4, b_cf], writes=[PB[3]])
            S.op("dve", lambda: nc.vector.tensor_copy(out=modT[:, :], in_=psum[3][:, 0:96]),
                 reads=[PB[3]], writes=[b_modT])
            SH1, SC1, G1, SH2, SC2, G2 = 0, 16, 32, 48, 64, 80
            S.op("dve", lambda: nc.vector.scalar_tensor_tensor(
                out=gsc[:, 0:16], in0=modT[:, SC1:SC1 + 16], scalar=1.0, in1=col(C_N1G, 16),
                op0=ALU.add, op1=ALU.mult), reads=[b_modT, b_cf], writes=[b_gsc])
            S.op("dve", lambda: nc.vector.scalar_tensor_tensor(
                out=gsc[:, 16:32], in0=modT[:, SC2:SC2 + 16], scalar=1.0, in1=col(C_N2G, 16),
                op0=ALU.add, op1=ALU.mult), reads=[b_modT, b_cf, b_gsc], writes=[b_gsc])
            S.op("dve", lambda: nc.vector.tensor_scalar(
                out=gsc[:, 32:33], in0=col(C_GQ), scalar1=1.0 / math.sqrt(128.0), scalar2=None, op0=ALU.mult),
                reads=[b_cf, b_gsc], writes=[b_gsc])
            S.op("pool", lambda: nc.gpsimd.memset(gsc[:, 33:34], 1.0), reads=[b_gsc], writes=[b_gsc])
            dbg("modT", modT[:, :], [b_modT])
            stop_at(1)

            def norm_mod_main(src, b_src, dst, b_dst, gofs, shofs, halves):
                for hf in halves:
                    sl = slice(hf * 512, (hf + 1) * 512)
                    pb = hf % 2
                    for fc in range(NFC):
                        k = fc % 2
                        S.op("act", (lambda fc=fc, k=k, sl=sl: nc.scalar.activation(
                            out=sqb[k][:, :], in_=src[:, fc, sl], func=AF.Square)),
                            reads=[b_src[fc]], writes=[b_sqb[k]])
                        S.op("pe", (lambda fc=fc, k=k, pb=pb: nc.tensor.matmul(
                            psum[pb][:, :], lhsT=ones_b, rhs=sqb[k][:, :],
                            start=(fc == 0), stop=(fc == NFC - 1))),
                            reads=[b_sqb[k], b_cb], writes=[PB[pb]])
                    S.op("act", (lambda pb=pb: nc.scalar.activation(
                        out=rstd[pb][:, :], in_=psum[pb][:, :], func=AF.Sqrt, scale=1.0 / D, bias=EPS)),
                        reads=[PB[pb]], writes=[b_rstd[pb]])
                    S.op("dve", (lambda pb=pb: nc.vector.reciprocal(out=rstd[pb][:, :], in_=rstd[pb][:, :])),
                         reads=[b_rstd[pb]], writes=[b_rstd[pb]])
                    for fc in range(NFC):
                        k = fc % 2
                        tmp = ybuf[k][:, 0:2, :].rearrange("p a b -> p (a b)")
                        S.op("dve", (lambda fc=fc, tmp=tmp, sl=sl, pb=pb: nc.vector.scalar_tensor_tensor(
                            out=tmp, in0=src[:, fc, sl], scalar=gsc[:, gofs + fc:gofs + fc + 1],
                            in1=rstd[pb][:, :], op0=ALU.mult, op1=ALU.mult)),
                            reads=[b_src[fc], b_gsc, b_rstd[pb]], writes=[b_ybuf[k]])
                        S.op("act", (lambda fc=fc, tmp=tmp, sl=sl: nc.scalar.activation(
                            out=dst[:, fc, sl], in_=tmp, func=AF.Identity,
                            bias=modT[:, shofs + fc:shofs + fc + 1], scale=1.0)),
                            reads=[b_ybuf[k], b_modT], writes=[b_dst[fc]])

            def norm_mod_halo(src, dst, gofs, shofs, mask):
                S.op("act", lambda: nc.scalar.activation(out=xhsq, in_=src, func=AF.Square),
                     reads=[b_R4], writes=[b_R4])

                def mm():
                    ins = None
                    for fc in range(NFC):
                        ins = nc.tensor.matmul(psum[2][:, 0:8], lhsT=ones_b, rhs=xhsq[:, fc, :],
                                               start=(fc == 0), stop=(fc == NFC - 1))
                    return ins
                S.op("pe", mm, reads=[b_R4, b_cb], writes=[PB[2]])
                S.op("act", lambda: nc.scalar.activation(
                    out=smallv[:, 0:8], in_=psum[2][:, 0:8], func=AF.Sqrt, scale=1.0 / D, bias=EPS),
                    reads=[PB[2]], writes=[b_small])
                S.op("dve", lambda: nc.vector.reciprocal(out=smallv[:, 8:16], in_=smallv[:, 0:8]),
                     reads=[b_small], writes=[b_small])
                S.op("dve", lambda: nc.vector.tensor_tensor(
                    out=xhtmp, in0=src, in1=gsc[:, gofs:gofs + 16].unsqueeze(2).broadcast_to([128, NFC, 8]),
                    op=ALU.mult), reads=[b_R4, b_gsc], writes=[b_R4])
                S.op("dve", lambda: nc.vector.tensor_tensor(
                    out=xhtmp, in0=xhtmp, in1=smallv[:, 8:16].unsqueeze(1).broadcast_to([128, NFC, 8]),
                    op=ALU.mult), reads=[b_R4, b_small], writes=[b_R4])
                S.op("dve", lambda: nc.vector.tensor_tensor(
                    out=xhtmp, in0=xhtmp,
                    in1=modT[:, shofs:shofs + 16].unsqueeze(2).broadcast_to([128, NFC, 8]),
                    op=ALU.add), reads=[b_R4, b_modT], writes=[b_R4])
                if mask:
                    S.op("dve", lambda: nc.vector.tensor_tensor(
                        out=dst[:, :, :], in0=xhtmp,
                        in1=cf[:, C_HM:C_HM + 8].unsqueeze(1).broadcast_to([128, NFC, 8]),
                        op=ALU.mult), reads=[b_R4, b_cf], writes=[b_hhT])
                else:
                    S.op("dve", lambda: nc.vector.tensor_copy(out=dst[:, :, :], in_=xhtmp),
                         reads=[b_R4], writes=[b_hhT])

            fine_R4 = b_ubuf + b_ybuf + [b_sg] + b_sqb + b_rstd
            att_R4 = b_pT + b_acc + [b_gm, b_top8, b_attn_n, b_km]
            alias(fine_R4, [b_R4])
            alias(b_KH + b_VH + b_mixC, [b_wa])
            S.dma("sp", lambda: nc.sync.dma_start(out=xhin, in_=xhT_d), writes=[b_R4], sem="xh")
            norm_mod_halo(xhin, hhT, 0, SH1, mask=False)
            norm_mod_main(xT, b_xT, hT, b_hT, 0, SH1, (0, 1))
            dbg("hT", hT, b_hT)
            stop_at(2)

            all_hT = list(b_hT)

            pstate = {"n": 0}

            def proj_tile(slot, rhs_fn, nbanks_ring, n=512, extra_reads=()):
                bank = pstate["n"] % nbanks_ring
                pstate["n"] += 1

                def mm():
                    ins = None
                    for fc in range(NFC):
                        ins = nc.tensor.matmul(psum[bank][:, 0:n], lhsT=R3[:, slot, fc, :], rhs=rhs_fn(fc),
                                               start=(fc == 0), stop=(fc == NFC - 1))
                    return ins
                S.op("pe", mm, reads=[b_ring[slot // 4]] + all_hT + list(extra_reads), writes=[PB[bank]])
                return bank

            def qk_epilogue(bank, hf, dstT, b_dst, gcol, ssb, is_k, h):
                k = ssb - 4
                S.op("act", lambda: nc.scalar.activation(out=sqb[k][:, :], in_=psum[bank][:, :], func=AF.Square),
                     reads=[PB[bank]], writes=[b_sqb[k]])
                S.op("pe", lambda: nc.tensor.matmul(psum[ssb][:, :], lhsT=ones_b, rhs=sqb[k][:, :],
                                                    start=True, stop=True),
                     reads=[b_sqb[k], b_cb], writes=[PB[ssb]])
                S.op("act", lambda: nc.scalar.activation(out=rstd[k][:, :], in_=psum[ssb][:, :], func=AF.Sqrt,
                                                         scale=1.0 / 128.0, bias=EPS),
                     reads=[PB[ssb]], writes=[b_rstd[k]])
                S.op("dve", lambda: nc.vector.reciprocal(out=rstd[k][:, :], in_=rstd[k][:, :]),
                     reads=[b_rstd[k]], writes=[b_rstd[k]])
                for blk in range(2):
                    i = hf * 2 + blk
                    sl = slice(i * 256, (i + 1) * 256)
                    ps = slice(blk * 256, (blk + 1) * 256)
                    if is_k:
                        S.op("dve", (lambda sl=sl, ps=ps, i=i: nc.vector.scalar_tensor_tensor(
                            out=dstT[:, h, sl], in0=psum[bank][:, ps], scalar=gcol, in1=rstd[k][:, ps],
                            op0=ALU.mult, op1=ALU.mult, accum_out=kmsum[:, h * 4 + i:h * 4 + i + 1])),
                            reads=[PB[bank], b_rstd[k], b_cf, b_gsc], writes=[b_dst, b_kmsum])
                    else:
                        S.op("dve", (lambda sl=sl, ps=ps: nc.vector.scalar_tensor_tensor(
                            out=dstT[:, h, sl], in0=psum[bank][:, ps], scalar=gcol, in1=rstd[k][:, ps],
                            op0=ALU.mult, op1=ALU.mult)),
                            reads=[PB[bank], b_rstd[k], b_cf, b_gsc], writes=[b_dst])

            n_groups_in = 12
            grp_of = {}
            next_load = {"g": 0}

            def ensure_loaded(dram, g, total, ahead=0):
                while next_load["g"] <= min(g + ahead, total - 1):
                    grp_of[next_load["g"]] = load_group(dram, next_load["g"])
                    next_load["g"] += 1
                return grp_of[min(g, total - 1)]

            ssb_state = {"n": 0}
            for h in range(NH):
                g = h // 4
                grp = ensure_loaded(win_d, g, n_groups_in)
                slot = grp * 4 + h % 4
                for hf in range(2):
                    bank = proj_tile(slot, (lambda fc, hf=hf: hT[:, fc, hf * 512:(hf + 1) * 512]), 4)
                    ssb = 4 + ssb_state["n"] % 2
                    ssb_state["n"] += 1
                    qk_epilogue(bank, hf, kT, b_kT[h], col(C_GK), ssb, True, h)
                ensure_loaded(win_d, g + 1, n_groups_in)
            dbg("kT", kT, b_kT)
            if STOP == 21:
                dbg("kmsum", kmsum[:, :], [b_kmsum])
            stop_at(21)
            S.op("pool", lambda: nc.gpsimd.memset(vA[:, :, :, 128:129], 1.0), writes=b_vA)
            for vh in range(2):
                g = 2 + vh
                grp = ensure_loaded(win_d, g, n_groups_in)
                for tt in range(8):
                    bank = pstate["n"] % 4
                    pstate["n"] += 1

                    def mm(tt=tt, grp=grp, bank=bank):
                        ins = None
                        for fc in range(NFC):
                            ins = nc.tensor.matmul(
                                psum[bank][:, :].rearrange("p (s m) -> p s m", s=4),
                                lhsT=hT[:, fc, tt * 128:(tt + 1) * 128],
                                rhs=R3[:, grp * 4:(grp + 1) * 4, fc, :],
                                start=(fc == 0), stop=(fc == NFC - 1))
                        return ins
                    S.op("pe", mm, reads=[b_ring[grp]] + all_hT, writes=[PB[bank]])
                    S.op("act", (lambda tt=tt, vh=vh, bank=bank: nc.scalar.activation(
                        out=vA[:, vh * 4:(vh + 1) * 4, tt, 0:128],
                        in_=psum[bank][:, :].rearrange("p (s m) -> p s m", s=4), func=AF.Copy)),
                        reads=[PB[bank]], writes=b_vA)
                ensure_loaded(win_d, g + 1, n_groups_in)
            if STOP == 22:
                dbg("vA", vA, b_vA)
            stop_at(22)
            S.op("dve", lambda: nc.vector.tensor_scalar(
                out=kmsum[:, :], in0=kmsum[:, :], scalar1=1.0 / 256.0, scalar2=None, op0=ALU.mult),
                reads=[b_kmsum], writes=[b_kmsum])
            for h_ in range(NH):
                S.dma("sp", (lambda h_=h_: nc.sync.dma_start(out=k_loc[h_].ap(), in_=kT[:, h_, :])),
                      reads=[b_kT[h_]], writes=[b_dram["k_loc%d" % h_]], sem="kst%d" % h_)
                S.dma("sp", (lambda h_=h_: nc.sync.dma_start(
                    out=v_loc[h_].ap().rearrange("p (t e) -> p t e", e=128), in_=vA[:, h_, :, 0:128])),
                    reads=b_vA, writes=[b_dram["v_loc%d" % h_]], sem="vst%d" % h_)
            S.dma("sp", lambda: nc.sync.dma_start(out=km_loc.ap(), in_=kmsum[:, :]),
                  reads=[b_kmsum], writes=[b_dram["km_loc"]], sem="kmst")
            cc_list = [("km", km_loc, km_g, "km_loc", "km_g")]
            for h_ in range(NH):
                cc_list.append(("k%d" % h_, k_loc[h_], k_g[h_], "k_loc%d" % h_, "k_g%d" % h_))
                cc_list.append(("v%d" % h_, v_loc[h_], v_g[h_], "v_loc%d" % h_, "v_g%d" % h_))
            for nm, src_t, dst_t, bs_, bd_ in cc_list:
                S.dma("pool", (lambda src_t=src_t, dst_t=dst_t: nc.gpsimd.collective_compute(
                    "AllGather", ALU.bypass, replica_groups=G4,
                    ins=[src_t.ap().opt()], outs=[dst_t.ap().opt()])),
                    reads=[b_dram[bs_]], writes=[b_dram[bd_]], sem="cc_" + nm, inc=1)

            dbg("vA", vA, b_vA)
            dbg("kmsum", kmsum[:, :], [b_kmsum])
            stop_at(3)
            alias(fine_R4, [b_R4])
            zbuf, yb, gcs = ubuf[0], ybuf[0], ybuf[1]
            b_z, b_yb, b_gcs = b_ubuf[0], b_ybuf[0], b_ybuf[1]
            b_hps = [Buf("hps%d" % i) for i in range(4)]
            for cc in range(8):
                tiles = {}
                for nm_i, nm in enumerate(("gc", "xt", "gb")):
                    flat = 16 + cc * 3 + nm_i
                    g = flat // 4
                    grp = ensure_loaded(win_d, g, n_groups_in)
                    tiles[nm] = grp * 4 + flat % 4
                hs = cc % 4
                for nm_i, nm in enumerate(("gc", "xt")):
                    def mmh(nm=nm, nm_i=nm_i, hs=hs, tiles=tiles):
                        ins = None
                        for fc in range(NFC):
                            ins = nc.tensor.matmul(psum[6][:, hs * 16 + nm_i * 8: hs * 16 + nm_i * 8 + 8],
                                                   lhsT=R3[:, tiles[nm], fc, :], rhs=hhT[:, fc, :],
                                                   start=(fc == 0), stop=(fc == NFC - 1))
                        return ins
                    S.op("pe", mmh, reads=[b_ring[tiles[nm] // 4], b_hhT], writes=[b_hps[hs]])
                S.op("act", (lambda hs=hs: nc.scalar.activation(
                    out=smallv[:, 16:24], in_=psum[6][:, hs * 16:hs * 16 + 8], func=AF.Copy)),
                    reads=[b_hps[hs]], writes=[b_small])
                S.op("dve", (lambda hs=hs: nc.vector.tensor_tensor(
                    out=smallv[:, 24:32], in0=psum[6][:, hs * 16 + 8:hs * 16 + 16], in1=smallv[:, 16:24],
                    op=ALU.mult)), reads=[b_hps[hs], b_small], writes=[b_small])
                S.op("dve", lambda: nc.vector.tensor_tensor(
                    out=zbuf[:, :, 0:2], in0=smallv[:, 24:32].rearrange("p (i t) -> p i t", t=2),
                    in1=cf[:, C_HM:C_HM + 8].rearrange("p (i t) -> p i t", t=2), op=ALU.mult),
                    reads=[b_small, b_cf], writes=[b_z])
                gb_banks = []
                for hf in range(2):
                    rh = (lambda fc, hf=hf: hT[:, fc, hf * 512:(hf + 1) * 512])
                    bk_gc = proj_tile(tiles["gc"], rh, 6)
                    bk_xt = proj_tile(tiles["xt"], rh, 6)
                    bk_gb = proj_tile(tiles["gb"], rh, 6)
                    gb_banks.append(bk_gb)
                    S.op("act", (lambda bk=bk_gc: nc.scalar.activation(
                        out=gcs[:, 0:2, :].rearrange("p a b -> p (a b)"), in_=psum[bk][:, :], func=AF.Copy)),
                        reads=[PB[bk_gc]], writes=[b_gcs])
                    S.op("dve", (lambda bk=bk_xt, hf=hf: nc.vector.tensor_tensor(
                        out=zbuf[:, 2 * hf:2 * hf + 2, 2:258],
                        in0=psum[bk][:, :].rearrange("p (a b) -> p a b", a=2),
                        in1=gcs[:, 0:2, :], op=ALU.mult)),
                        reads=[PB[bk_xt], b_gcs], writes=[b_z])
                S.op("act", (lambda cc=cc: nc.scalar.activation(
                    out=yb[:, :, :], in_=zbuf[:, :, 2:258], func=AF.Identity,
                    scale=col(C_CW + cc * 3 + 2), bias=col(C_CB + cc))),
                    reads=[b_z, b_cf], writes=[b_yb])
                S.op("dve", (lambda cc=cc: nc.vector.scalar_tensor_tensor(
                    out=yb[:, :, :], in0=zbuf[:, :, 1:257], scalar=col(C_CW + cc * 3 + 1), in1=yb[:, :, :],
                    op0=ALU.mult, op1=ALU.add)), reads=[b_z, b_cf, b_yb], writes=[b_yb])
                S.op("dve", (lambda cc=cc: nc.vector.scalar_tensor_tensor(
                    out=yb[:, :, :], in0=zbuf[:, :, 0:256], scalar=col(C_CW + cc * 3), in1=yb[:, :, :],
                    op0=ALU.mult, op1=ALU.add)), reads=[b_z, b_cf, b_yb], writes=[b_yb])
                for hf in range(2):
                    S.op("dve", (lambda cc=cc, hf=hf, bk=gb_banks[hf]: nc.vector.tensor_tensor(
                        out=mixC[:, cc, hf * 512:(hf + 1) * 512].rearrange("p (a b) -> p a b", a=2),
                        in0=psum[bk][:, :].rearrange("p (a b) -> p a b", a=2),
                        in1=yb[:, 2 * hf:2 * hf + 2, :], op=ALU.mult)),
                        reads=[PB[gb_banks[hf]], b_yb], writes=[b_mixC[cc]])
                ensure_loaded(win_d, (16 + cc * 3 + 2) // 4 + 1, n_groups_in)
            dbg("mixC", mixC, b_mixC)

            for h in range(NH):
                flat = 40 + h
                g = flat // 4
                grp = ensure_loaded(win_d, g, n_groups_in)
                slot = grp * 4 + flat % 4
                for hf in range(2):
                    bank = proj_tile(slot, (lambda fc, hf=hf: hT[:, fc, hf * 512:(hf + 1) * 512]), 4)
                    ssb = 4 + ssb_state["n"] % 2
                    ssb_state["n"] += 1
                    qk_epilogue(bank, hf, qT, b_qT[h], gsc[:, 32:33], ssb, False, h)
                ensure_loaded(win_d, g + 1, n_groups_in)
            dbg("qT", qT, b_qT)
            stop_at(4)

            next_load["g"] = 0
            ring_state["w"] = "w_out_t"
            grp_of.clear()
            if wout_d is not None:
                ensure_loaded(wout_d, 0, 4, ahead=1)

            alias(att_R4, fine_R4 + [b_R4])
            alias([PB[6]], b_hps)
            S.dma("sp", lambda: nc.sync.dma_start(
                out=kmall, in_=km_g.ap().rearrange("(r d) f -> d r f", d=128)),
                reads=[b_dram["km_g"]], writes=[b_km], sem="kmld")
            S.op("dve", lambda: nc.vector.tensor_copy(
                out=kmT2.rearrange("p h (i r) -> p h i r", r=4),
                in_=kmall.rearrange("p r (h i) -> p h i r", h=8)),
                reads=[b_km], writes=[b_km])
            S.op("dve", lambda: nc.vector.tensor_copy(out=kmhi, in_=kmT2), reads=[b_km], writes=[b_km])
            S.op("dve", lambda: nc.vector.tensor_tensor(out=kmtmp, in0=kmT2, in1=kmhi, op=ALU.subtract),
                 reads=[b_km], writes=[b_km])
            S.op("dve", lambda: nc.vector.tensor_copy(out=kmlo, in_=kmtmp), reads=[b_km], writes=[b_km])
            for tt in range(8):
                i = tt // 2
                gbank = 6 + tt % 2

                def mmg(tt=tt, gbank=gbank):
                    ins = None
                    for h in range(NH):
                        nc.tensor.matmul(psum[gbank][:, h * 16:(h + 1) * 16],
                                         lhsT=qT[:, h, tt * 128:(tt + 1) * 128], rhs=kmhi[:, h, :],
                                         start=True, stop=False)
                        ins = nc.tensor.matmul(psum[gbank][:, h * 16:(h + 1) * 16],
                                               lhsT=qT[:, h, tt * 128:(tt + 1) * 128], rhs=kmlo[:, h, :],
                                               start=False, stop=True)
                    return ins
                S.op("pe", mmg, reads=[b_km] + b_R1[0:4], writes=[PB[gbank]])
                S.op("dve", (lambda i=i, gbank=gbank: nc.vector.tensor_tensor(
                    out=gm, in0=psum[gbank][:, 0:128].rearrange("p (h k) -> p h k", h=8),
                    in1=cf[:, C_PB + i * 16:C_PB + i * 16 + 16].unsqueeze(1).broadcast_to([128, NH, 16]),
                    op=ALU.add)), reads=[PB[gbank], b_cf], writes=[b_gm])
                for h in range(NH):
                    S.op("dve", (lambda h=h: nc.vector.max(out=top8[:, h, :], in_=gm[:, h, :])),
                         reads=[b_gm], writes=[b_top8])
                S.op("dve", lambda: nc.vector.tensor_tensor(
                    out=gm, in0=gm, in1=top8[:, :, 2:3].broadcast_to([128, NH, 16]), op=ALU.is_ge),
                    reads=[b_gm, b_top8], writes=[b_gm])
                S.op("dve", (lambda tt=tt, i=i: nc.vector.tensor_tensor(
                    out=selb[:, tt, :, :], in0=gm,
                    in1=cf[:, C_P01 + i * 16:C_P01 + i * 16 + 16].unsqueeze(1).broadcast_to([128, NH, 16]),
                    op=ALU.mult)), reads=[b_gm, b_cf], writes=b_sel)
            dbg("sel", selb, b_sel)
            stop_at(5)

            b_attn = [[b_R2[2 * tt], b_R2[2 * tt + 1]] for tt in range(8)]
            for s2 in range(2):
                S.op("pool", (lambda s2=s2: nc.gpsimd.memset(VH[s2][:, :, :, 128:129], 1.0)),
                     writes=[b_VH[s2]])
            slot_n = {"n": 0}

            def load_head(h):
                s2 = h % 2
                for r in range(4):
                    S.dma("sp", (lambda r=r, h=h, s2=s2: nc.sync.dma_start(
                        out=KH[s2][:, r * 4:(r + 1) * 4, :].rearrange("p i t -> p (i t)"),
                        in_=k_g[h].ap()[r * 128:(r + 1) * 128, :])),
                        reads=[b_dram["k_g%d" % h]], writes=[b_KH[s2]], sem="kh%d" % s2)
                    S.dma("sp", (lambda r=r, h=h, s2=s2: nc.sync.dma_start(
                        out=VH[s2][:, r * 4:(r + 1) * 4, :, 0:128].rearrange("p i c e -> p (i c) e"),
                        in_=v_g[h].ap()[r * 128:(r + 1) * 128, :].rearrange(
                            "p (ic e) -> p ic e", e=128))),
                        reads=[b_dram["v_g%d" % h]], writes=[b_VH[s2]], sem="vh%d" % s2)

            load_head(0)
            if STOP == 61:
                load_head(1)
            stop_at(61)
            for h in range(NH):
                if h + 1 < NH:
                    load_head(h + 1)
                s2 = h % 2
                for i in range(4):
                    a2 = i % 2
                    slots = [("d", None)] + [("p", kb) for kb in range(4 * i + 3)]
                    for kind, kb in slots:
                        n = slot_n["n"]
                        slot_n["n"] += 1
                        sb_, ob_, p2 = n % 2, 2 + n % 2, n % 2

                        def mms(kind=kind, kb=kb, i=i, h=h, s2=s2, sb_=sb_):
                            ins = None
                            for kc in range(2):
                                if kind == "d":
                                    lt = kT[:, h, i * 256 + kc * 128:i * 256 + (kc + 1) * 128]
                                else:
                                    lt = KH[s2][:, (kb % 4) * 4 + kb // 4, kc * 128:(kc + 1) * 128]
                                ins = nc.tensor.matmul(psum[sb_][:, kc * 256:(kc + 1) * 256], lhsT=lt,
                                                       rhs=qT[:, h, i * 256:(i + 1) * 256],
                                                       start=True, stop=True)
                            return ins
                        rd = [b_qT[h]] + ([b_kT[h]] if kind == "d" else [b_KH[s2]])
                        S.op("pe", mms, reads=rd, writes=[PB[sb_]])
                        S.op("act", (lambda sb_=sb_, p2=p2: nc.scalar.activation(
                            out=pT[p2], in_=psum[sb_][:, :].rearrange("p (c q) -> p c q", c=2), func=AF.Exp)),
                            reads=[PB[sb_]], writes=[b_pT[p2]])
                        if kind == "d":
                            S.op("dve", (lambda p2=p2: nc.vector.tensor_tensor(
                                out=pT[p2], in0=pT[p2], in1=tri, op=ALU.mult)),
                                reads=[b_pT[p2], b_cb], writes=[b_pT[p2]])

                        def mmo(kind=kind, kb=kb, i=i, h=h, s2=s2, ob_=ob_, p2=p2):
                            ins = None
                            for qh in range(2):
                                for kc in range(2):
                                    if kind == "d":
                                        rv = vA[:, h, 2 * i + kc, 0:129]
                                    else:
                                        rv = VH[s2][:, (kb % 4) * 4 + kb // 4, kc, 0:129]
                                    ins = nc.tensor.matmul(psum[ob_][:, qh * 256:qh * 256 + 129],
                                                           lhsT=pT[p2][:, kc, qh * 128:(qh + 1) * 128], rhs=rv,
                                                           start=(kc == 0), stop=(kc == 1))
                            return ins
                        rd = [b_pT[p2]] + (list(b_vA) if kind == "d" else [b_VH[s2]])
                        S.op("pe", mmo, reads=rd, writes=[PB[ob_]])
                        if kind == "d":
                            S.op("dve", (lambda ob_=ob_, a2=a2: nc.vector.tensor_copy(
                                out=acc[a2][:, :, 0:129],
                                in_=psum[ob_][:, :].rearrange("p (a b) -> p a b", a=2)[:, :, 0:129])),
                                reads=[PB[ob_]], writes=[b_acc[a2]])
                        else:
                            for qh in range(2):
                                S.op("dve", (lambda ob_=ob_, a2=a2, qh=qh, kb=kb, i=i, h=h:
                                             nc.vector.scalar_tensor_tensor(
                                    out=acc[a2][:, qh, 0:129], in0=psum[ob_][:, qh * 256:qh * 256 + 129],
                                    scalar=selb[:, 2 * i + qh, h, kb:kb + 1], in1=acc[a2][:, qh, 0:129],
                                    op0=ALU.mult, op1=ALU.add)),
                                    reads=[PB[ob_], b_acc[a2]] + b_sel, writes=[b_acc[a2]])
                    S.op("dve", (lambda a2=a2: nc.vector.reciprocal(
                        out=acc[a2][:, :, 130:131], in_=acc[a2][:, :, 128:129])),
                        reads=[b_acc[a2]], writes=[b_acc[a2]])
                    for qh in range(2):
                        tt = 2 * i + qh
                        S.op("act", (lambda a2=a2, qh=qh, tt=tt, h=h: nc.scalar.activation(
                            out=attn[:, tt, h * 128:(h + 1) * 128], in_=acc[a2][:, qh, 0:128],
                            func=AF.Copy, scale=acc[a2][:, qh, 130:131])),
                            reads=[b_acc[a2]], writes=b_attn[tt])
            dbg("attn", attn, b_R2)
            stop_at(62)

            alias(b_mixA, b_KH + b_VH)
            for tt in range(8):
                S.op("dve", (lambda tt=tt: nc.vector.scalar_tensor_tensor(
                    out=attn_n, in0=attn[:, tt, :], scalar=gsc[:, 33:34], in1=attn[:, tt, :],
                    op0=ALU.mult, op1=ALU.mult, accum_out=smallv[:, 32 + tt:33 + tt])),
                    reads=b_attn[tt] + [b_gsc], writes=[b_attn_n, b_small])
            S.op("act", lambda: nc.scalar.activation(
                out=smallv[:, 40:48], in_=smallv[:, 32:40], func=AF.Sqrt,
                scale=1.0 / 1024.0, bias=EPS), reads=[b_small], writes=[b_small])
            S.op("dve", lambda: nc.vector.reciprocal(out=smallv[:, 48:56], in_=smallv[:, 40:48]),
                 reads=[b_small], writes=[b_small])
            for tt in range(8):
                S.op("dve", (lambda tt=tt: nc.vector.tensor_scalar(
                    out=attn[:, tt, :], in0=attn[:, tt, :], scalar1=smallv[:, 48 + tt:49 + tt], scalar2=None,
                    op0=ALU.mult)), reads=b_attn[tt] + [b_small], writes=b_attn[tt])
                for half in range(2):
                    tb = 4 + half

                    def mmt(tt=tt, half=half, tb=tb):
                        ins = None
                        for h4 in range(4):
                            hh = half * 4 + h4
                            ins = nc.tensor.transpose(psum[tb][:, h4 * 128:(h4 + 1) * 128],
                                                      attn[:, tt, hh * 128:(hh + 1) * 128], ident_f)
                        return ins
                    S.op("pe", mmt, reads=b_attn[tt] + [b_cf], writes=[PB[tb]])
                    for h4 in range(4):
                        hh = half * 4 + h4
                        if False:
                            S.op("act", (lambda hh=hh, h4=h4, tt=tt, tb=tb: nc.scalar.activation(
                                out=mixA[:, hh, tt * 128:(tt + 1) * 128], in_=psum[tb][:, h4 * 128:(h4 + 1) * 128],
                                func=AF.Copy, scale=col(C_AG + hh))),
                                reads=[PB[tb], b_cf], writes=[b_mixA[hh]])
                        else:
                            S.op("dve", (lambda hh=hh, h4=h4, tt=tt, tb=tb: nc.vector.tensor_scalar(
                                out=mixA[:, hh, tt * 128:(tt + 1) * 128], in0=psum[tb][:, h4 * 128:(h4 + 1) * 128],
                                scalar1=col(C_AG + hh), scalar2=None, op0=ALU.mult)),
                                reads=[PB[tb], b_cf], writes=[b_mixA[hh]])

            if STOP == 63:
                dbg("mixA", mixA, b_mixA)
            stop_at(63)
            for hf in range(2):
                sl = slice(hf * 512, (hf + 1) * 512)
                pb = hf
                for cc in range(8):
                    k = cc % 2
                    S.op("act", (lambda cc=cc, k=k, sl=sl: nc.scalar.activation(
                        out=sqb[k][:, :], in_=mixC[:, cc, sl], func=AF.Square)),
                        reads=[b_mixC[cc]], writes=[b_sqb[k]])
                    S.op("pe", (lambda cc=cc, k=k, pb=pb: nc.tensor.matmul(
                        psum[pb][:, :], lhsT=ones_b, rhs=sqb[k][:, :], start=(cc == 0), stop=(cc == 7))),
                        reads=[b_sqb[k], b_cb], writes=[PB[pb]])
                S.op("act", (lambda pb=pb: nc.scalar.activation(
                    out=rstd[0][:, :], in_=psum[pb][:, :], func=AF.Sqrt, scale=1.0 / 1024.0, bias=EPS)),
                    reads=[PB[pb]], writes=[b_rstd[0]])
                S.op("dve", lambda: nc.vector.reciprocal(out=rstd[0][:, :], in_=rstd[0][:, :]),
                     reads=[b_rstd[0]], writes=[b_rstd[0]])
                for cc in range(8):
                    S.op("dve", (lambda cc=cc, sl=sl: nc.vector.scalar_tensor_tensor(
                        out=mixC[:, cc, sl], in0=mixC[:, cc, sl], scalar=col(C_CG + cc), in1=rstd[0][:, :],
                        op0=ALU.mult, op1=ALU.mult)),
                        reads=[b_mixC[cc], b_cf, b_rstd[0]], writes=[b_mixC[cc]])
            dbg("mixA", mixA, b_mixA)
            dbg("mixC2", mixC, b_mixC)
            stop_at(6)

            xm = xT
            b_xm = b_R1
            xst = [sg.rearrange("p a b -> p (a b)")[:, 0:512], sg.rearrange("p a b -> p (a b)")[:, 512:1024]]
            b_xst = [Buf("xst0"), Buf("xst1")]
            alias(b_xst, [b_sg, b_sqb[0], b_sqb[1], b_rstd[0]])
            pstate["n"] = 0
            for ot in range(16):
                g = ot // 4
                grp = ensure_loaded(wout_d, g, 4)
                slot = grp * 4 + ot % 4
                for hf in range(2):
                    sl = slice(hf * 512, (hf + 1) * 512)
                    k = (ot * 2 + hf) % 2
                    S.dma("sp", (lambda ot=ot, sl=sl, k=k: nc.sync.dma_start(out=xst[k], in_=xT_d[:, ot, sl])),
                          writes=[b_xst[k]], sem="xst%d" % k)
                    bank = pstate["n"] % 4
                    pstate["n"] += 1

                    def mm(slot=slot, sl=sl, bank=bank):
                        ins = None
                        for mc in range(16):
                            rhs = mixA[:, mc, sl] if mc < 8 else mixC[:, mc - 8, sl]
                            ins = nc.tensor.matmul(psum[bank][:, :], lhsT=R3[:, slot, mc, :], rhs=rhs,
                                                   start=(mc == 0), stop=(mc == 15))
                        return ins
                    S.op("pe", mm, reads=[b_ring[slot // 4]] + b_mixA + b_mixC, writes=[PB[bank]])
                    S.op("dve", (lambda ot=ot, sl=sl, k=k, bank=bank: nc.vector.scalar_tensor_tensor(
                        out=xm[:, ot, sl], in0=psum[bank][:, :], scalar=modT[:, G1 + ot:G1 + ot + 1],
                        in1=xst[k], op0=ALU.mult, op1=ALU.add)),
                        reads=[PB[bank], b_modT, b_xst[k]], writes=[b_xm[ot]])
                if g + 1 < 4:
                    ensure_loaded(wout_d, g + 1, 4)
            dbg("xmid", xm, b_xm)
            stop_at(7)

            alias([b_R4], att_R4 + fine_R4 + b_xst)
            S.op("dve", lambda: nc.vector.tensor_copy(
                out=xhpay, in_=xm.rearrange("p f (i t) -> p f i t", i=4)[:, :, :, 254:256]),
                reads=b_xm, writes=[b_R4])
            S.dma("sp", lambda: nc.sync.dma_start(
                out=xh_loc.ap(), in_=xhpay.rearrange("p f i t -> p (f i t)")),
                reads=[b_R4], writes=[b_dram["xh_loc"]], sem="xhst")
            S.dma("pool", lambda: nc.gpsimd.collective_compute(
                "AllGather", ALU.bypass, replica_groups=G4,
                ins=[xh_loc.ap().opt()], outs=[xh_g.ap().opt()]),
                reads=[b_dram["xh_loc"]], writes=[b_dram["xh_g"]], sem="cc_xh", inc=1)
            S.dma("sp", lambda: nc.sync.dma_start(
                out=xhg.rearrange("p r f i t -> p r (f i t)"),
                in_=xh_g.ap().rearrange("(r p) x -> p r x", p=128)),
                reads=[b_dram["xh_g"]], writes=[b_R4], sem="xhld")
            S.op("dve", lambda: nc.vector.tensor_scalar(
                out=xh2, in0=xhg[:, 0, :, :, :], scalar1=col(C_SELW), scalar2=None, op0=ALU.mult),
                reads=[b_R4, b_cf], writes=[b_R4])
            for r in (1, 2):
                S.op("dve", (lambda r=r: nc.vector.scalar_tensor_tensor(
                    out=xh2, in0=xhg[:, r, :, :, :], scalar=col(C_SELW + r), in1=xh2,
                    op0=ALU.mult, op1=ALU.add)), reads=[b_R4, b_cf], writes=[b_R4])
            S.op("dve", lambda: nc.vector.scalar_tensor_tensor(
                out=xh2[:, :, 1:4, :], in0=xhg[:, 3, :, 0:3, :], scalar=col(C_SELW + 3), in1=xh2[:, :, 1:4, :],
                op0=ALU.mult, op1=ALU.add), reads=[b_R4, b_cf], writes=[b_R4])
            S.op("dve", lambda: nc.vector.tensor_copy(
                out=xhin.rearrange("p f (i t) -> p f i t", t=2), in_=xh2), reads=[b_R4], writes=[b_R4])
            b_h2T = b_R2
            norm_mod_halo(xhin, hhT, 16, SH2, mask=True)
            alias(fine_R4, [b_R4] + att_R4 + b_xst)
            norm_mod_main(xm, b_xm, hT, b_h2T, 16, SH2, (0, 1))
            dbg("h2T", hT, b_h2T)
            stop_at(8)

            b_fh = [Buf("fh%d" % i) for i in range(8)]
            alias(fine_R4, [b_R4] + fine_R4)
            alias([b_act] + b_WD, b_mixA + b_mixC + b_KH + b_VH)
            alias(b_fh, [PB[4]])
            next_load["g"] = 0
            ring_state["w"] = "w_up_t"
            grp_of.clear()
            pstate["n"] = 0
            dstate = {"n": 0}
            wdq = {"n": 0}
            out_toks = []
            for ch in range(NCHUNK):
                for fl in range(CHUNK):
                    f = ch * CHUNK + fl
                    for part in range(2):
                        flat = f * 2 + part
                        g = flat // 4
                        grp = ensure_loaded(wup_d, g, 22)
                        slot = grp * 4 + flat % 4
                        tix = part * NFF + f
                        hs = flat % 8
                        ub, b_ub = ubuf[part], b_ubuf[part]
                        ybp, b_ybp = ybuf[part], b_ybuf[part]

                        def mmh(slot=slot, hs=hs):
                            ins = None
                            for fc in range(NFC):
                                ins = nc.tensor.matmul(psum[4][:, hs * 8:hs * 8 + 8], lhsT=R3[:, slot, fc, :],
                                                       rhs=hhT[:, fc, :], start=(fc == 0), stop=(fc == NFC - 1))
                            return ins
                        S.op("pe", mmh, reads=[b_ring[slot // 4], b_hhT], writes=[b_fh[hs]])
                        S.op("act", (lambda hs=hs, ub=ub: nc.scalar.activation(
                            out=ub[:, :, 0:2], in_=psum[4][:, hs * 8:hs * 8 + 8].rearrange("p (i t) -> p i t", t=2),
                            func=AF.Copy)), reads=[b_fh[hs]], writes=[b_ub])
                        for hf in range(2):
                            bank = proj_tile(slot, (lambda fc, hf=hf: hT[:, fc, hf * 512:(hf + 1) * 512]), 4)
                            S.op("act", (lambda bank=bank, hf=hf, ub=ub: nc.scalar.activation(
                                out=ub[:, 2 * hf:2 * hf + 2, 2:258],
                                in_=psum[bank][:, :].rearrange("p (a b) -> p a b", a=2), func=AF.Copy)),
                                reads=[PB[bank]], writes=[b_ub])
                        S.op("act", (lambda tix=tix, ub=ub, ybp=ybp: nc.scalar.activation(
                            out=ybp[:, :, :], in_=ub[:, :, 2:258], func=AF.Identity,
                            scale=col(C_FW + tix * 3 + 2), bias=col(C_FB + tix))),
                            reads=[b_ub, b_cf], writes=[b_ybp])
                        S.op("dve", (lambda tix=tix, ub=ub, ybp=ybp: nc.vector.scalar_tensor_tensor(
                            out=ybp[:, :, :], in0=ub[:, :, 1:257], scalar=col(C_FW + tix * 3 + 1), in1=ybp[:, :, :],
                            op0=ALU.mult, op1=ALU.add)), reads=[b_ub, b_cf, b_ybp], writes=[b_ybp])
                        S.op("dve", (lambda tix=tix, ub=ub, ybp=ybp: nc.vector.scalar_tensor_tensor(
                            out=ybp[:, :, :], in0=ub[:, :, 0:256], scalar=col(C_FW + tix * 3), in1=ybp[:, :, :],
                            op0=ALU.mult, op1=ALU.add)), reads=[b_ub, b_cf, b_ybp], writes=[b_ybp])
                        if g + 1 < 22:
                            ensure_loaded(wup_d, g + 1, 22)
                    S.op("act", lambda: nc.scalar.activation(out=sg[:, :, :], in_=ybuf[0][:, :, :], func=AF.Silu),
                         reads=[b_ybuf[0]], writes=[b_sg])
                    S.op("dve", (lambda fl=fl: nc.vector.tensor_tensor(
                        out=actT[:, fl, :].rearrange("p (a b) -> p a b", a=4), in0=sg[:, :, :], in1=ybuf[1][:, :, :],
                        op=ALU.mult)), reads=[b_sg, b_ybuf[1]], writes=[b_act])
                for qd in range(4):
                    w2 = wdq["n"] % 2
                    wdq["n"] += 1
                    S.dma("pool", (lambda ch=ch, qd=qd, w2=w2: nc.gpsimd.dma_start(
                        out=WD[w2],
                        in_=wdn_d[ch * CHUNK:(ch + 1) * CHUNK, :, qd * 512:(qd + 1) * 512].rearrange("f p x -> p f x"))),
                        reads=[b_wf["w_down_t"]], writes=[b_WD[w2]], sem="wd%d" % w2)
                    for o4 in range(4):
                        ot = qd * 4 + o4
                        for hf in range(2):
                            sl = slice(hf * 512, (hf + 1) * 512)
                            bank = 5 + dstate["n"] % 3
                            dstate["n"] += 1

                            def mmd(w2=w2, o4=o4, sl=sl, bank=bank):
                                ins = None
                                for fl in range(CHUNK):
                                    ins = nc.tensor.matmul(psum[bank][:, :], lhsT=WD[w2][:, fl, o4 * 128:(o4 + 1) * 128],
                                                           rhs=actT[:, fl, sl], start=(fl == 0), stop=(fl == CHUNK - 1))
                                return ins
                            S.op("pe", mmd, reads=[b_WD[w2], b_act], writes=[PB[bank]])
                            S.op("dve", (lambda ot=ot, sl=sl, bank=bank: nc.vector.scalar_tensor_tensor(
                                out=xm[:, ot, sl], in0=psum[bank][:, :], scalar=modT[:, G2 + ot:G2 + ot + 1],
                                in1=xm[:, ot, sl], op0=ALU.mult, op1=ALU.add)),
                                reads=[PB[bank], b_modT, b_xm[ot]], writes=[b_xm[ot]])
                        if ch == NCHUNK - 1:
                            out_toks.append(S.dma("sp", (lambda ot=ot: nc.sync.dma_start(
                                out=out_d[:, ot, :], in_=xm[:, ot, :])), reads=[b_xm[ot]], sem="outst"))

        except _Stop:
            pass
        S.final_wait("sp", [(n_, v_) for n_, v_ in S.dma_sems.items()])
        sem_names = list(Sched.ENG) + list(S.dma_sems.keys())
        sems = {n: es.enter_context(nc.semaphore("s_" + n)) for n in sem_names}
        block = es.enter_context(nc.Block())

        def run(engname):
            eng = E[engname]
            for waits, fn, inc in S.prog[engname]:
                for s, v in waits:
                    eng.wait_ge(sems[s], v)
                if fn is not None:
                    ins = fn()
                    ins.then_inc(sems[inc[0]], inc[1])

        @block.tensor
        def _(e):
            run("pe")

        @block.scalar
        def _(e):
            run("act")

        @block.vector
        def _(e):
            run("dve")

        @block.gpsimd
        def _(e):
            run("pool")

        @block.sync
        def _(e):
            run("sp")
    _BUILD_INFO["declared"] = declared
    return nc


def _pm(v, n):
    return np.ascontiguousarray(np.asarray(v, np.float32).reshape(n, 128).T)


def _wtiles(w, cols):
    K = w.shape[0]
    nk = K // 128
    w4 = w.reshape(nk, 128, w.shape[1] // 128, 128)
    w4 = w4[:, :, cols, :]
    return np.ascontiguousarray(w4.transpose(2, 1, 0, 3)).reshape(len(cols), 128, nk * 128)


_NC_CACHE = {}
_BUILD_INFO = {}


def kernel(x, c, w_ada, b_ada, norm1_g, w_in, q_norm_g, k_norm_g, conv_w, conv_b,
           attn_out_g, conv_out_g, w_out, norm2_g, w_ffn_up, ffn_conv_w, ffn_conv_b, w_ffn_down):
    x = np.asarray(x, np.float32)
    c = np.asarray(c, np.float32)
    w_ada = np.asarray(w_ada, np.float32)[0]
    b_ada = np.asarray(b_ada, np.float32)[0]
    w_in = np.asarray(w_in, np.float32)[0]
    w_out = np.asarray(w_out, np.float32)[0]
    w_up = np.asarray(w_ffn_up, np.float32)[0]
    w_dn = np.asarray(w_ffn_down, np.float32)[0]

    order = list(range(8, 16)) + list(range(16, 24))
    for cc in range(8):
        order += [32 + cc, 40 + cc, 24 + cc]
    order += list(range(0, 8))
    w_in_t = _wtiles(w_in, order)
    w_out_t = _wtiles(w_out, list(range(16)))
    up_order = []
    for f in range(NFF):
        up_order += [f, NFF + f]
    w_up_t = _wtiles(w_up, up_order)
    w_down_t = np.ascontiguousarray(w_dn.reshape(NFF, 128, 2048))


    cf0 = np.zeros((128, NCF), np.float32)
    cf0[:, C_N1G:C_N1G + 16] = _pm(np.asarray(norm1_g)[0], 16)
    cf0[:, C_N2G:C_N2G + 16] = _pm(np.asarray(norm2_g)[0], 16)
    cf0[:, C_GQ] = np.asarray(q_norm_g, np.float32)[0]
    cf0[:, C_GK] = np.asarray(k_norm_g, np.float32)[0]
    cw = np.asarray(conv_w, np.float32)[0]
    cf0[:, C_CW:C_CW + 24] = cw.reshape(3, 8, 128).transpose(2, 1, 0).reshape(128, 24)
    cf0[:, C_CB:C_CB + 8] = _pm(np.asarray(conv_b)[0], 8)
    cf0[:, C_AG:C_AG + 8] = _pm(np.asarray(attn_out_g)[0], 8)
    cf0[:, C_CG:C_CG + 8] = _pm(np.asarray(conv_out_g)[0], 8)
    fw = np.asarray(ffn_conv_w, np.float32)[0]
    cf0[:, C_FW:C_FW + 264] = fw.reshape(3, 88, 128).transpose(2, 1, 0).reshape(128, 264)
    cf0[:, C_FB:C_FB + 88] = _pm(np.asarray(ffn_conv_b)[0], 88)
    cf0[:, C_ID:C_ID + 128] = np.eye(128, dtype=np.float32)

    cb0 = np.zeros((128, NCB), np.float32)
    cb0[:, B_ID:B_ID + 128] = np.eye(128)
    cb0[:, B_ONES:B_ONES + 128] = 1.0
    kk = np.arange(128)[:, None, None] + 128 * np.arange(2)[None, :, None]
    qq = np.arange(256)[None, None, :]
    cb0[:, B_TRI:B_TRI + 512] = (kk <= qq).astype(np.float32).reshape(128, 512)
    cb16 = cb0.astype(ml_dtypes.bfloat16)

    in_maps = []
    tok_lists = []
    for core in range(8):
        b, j = core // 4, core % 4
        toks = np.concatenate([np.arange(256 * (4 * i + j), 256 * (4 * i + j) + 256) for i in range(4)])
        tok_lists.append(toks)
        xl = x[b, toks, :]
        xTl = np.ascontiguousarray(xl.T.reshape(NFC, 128, TOK).transpose(1, 0, 2))
        xh = np.zeros((8, D), np.float32)
        hm = np.ones((8,), np.float32)
        for i in range(4):
            qb = 4 * i + j
            if qb == 0:
                hm[2 * i:2 * i + 2] = 0.0
            else:
                xh[2 * i:2 * i + 2] = x[b, 256 * qb - 2:256 * qb, :]
        xhT = np.ascontiguousarray(xh.T.reshape(NFC, 128, 8).transpose(1, 0, 2))
        cfc = cf0.copy()
        pb = np.zeros((4, 16), np.float32)
        p01 = np.zeros((4, 16), np.float32)
        for i in range(4):
            qb = 4 * i + j
            pb[i, qb:] = NEG
            p01[i, :qb] = 1.0
        cfc[:, C_PB:C_PB + 64] = pb.reshape(1, 64)
        cfc[:, C_P01:C_P01 + 64] = p01.reshape(1, 64)
        selw = np.zeros(4, np.float32)
        selw[(j - 1) % 4] = 1.0
        cfc[:, C_SELW:C_SELW + 4] = selw[None, :]
        bsel = np.zeros(2, np.float32)
        bsel[b] = 1.0
        cfc[:, C_BSEL:C_BSEL + 2] = bsel[None, :]
        cfc[:, C_HM:C_HM + 8] = hm[None, :]
        wsl = w_ada[:, j * 3072:(j + 1) * 3072]
        wada = np.ascontiguousarray(wsl.reshape(NFC, 128, 2, 1536).transpose(1, 2, 0, 3))
        bada = np.ascontiguousarray(np.broadcast_to(b_ada[j * 3072:(j + 1) * 3072][None, :], (2, 3072)))
        cT = np.ascontiguousarray(np.broadcast_to(c[b].reshape(NFC, 128).T[:, :, None], (128, NFC, 2)))
        in_maps.append({
            "xT": xTl, "xhT": xhT, "cT": cT, "wada": wada, "bada": bada, "cf": cfc, "cb16": cb16,
            "w_in_t": w_in_t.reshape(-1, 2048), "w_out_t": w_out_t.reshape(-1, 2048),
            "w_up_t": w_up_t.reshape(-1, 2048), "w_down_t": w_down_t.reshape(-1, 2048),
        })

    if "nc" not in _NC_CACHE:
        _NC_CACHE["nc"] = build_program()
        _NC_CACHE["declared"] = list(_BUILD_INFO["declared"])
    nc = _NC_CACHE["nc"]
    keep = set(_NC_CACHE["declared"])
    in_maps = [{k: v for k, v in m.items() if k in keep} for m in in_maps]
    res = run_bass_kernel_spmd(nc, in_maps, core_ids=list(range(8)))
    out = np.empty((2, 4096, D), np.float32)
    for core in range(8):
        b = core // 4
        oT = np.asarray(res.results[core]["outT"])
        out[b, tok_lists[core], :] = oT.transpose(2, 1, 0).reshape(TOK, D)
    kernel.last_results = res
    return out
```
